# Optimizing a Trainium2 kernel written in Bass

```python
import jax, jax.numpy as jnp
from jax import lax
import numpy as np

D_MODEL = 1024
BATCH = 8
SEQ = 2048
DEPTH = 2

GRID_W = 64
CTX_LEN = 256
N_MIXERS = 2
N_ATTN_LAYERS = (DEPTH + 1) // 2
N_SSD_LAYERS = DEPTH // 2
EPS = 1e-6

HEAD_DIM = 64
N_HEADS = D_MODEL // HEAD_DIM
N_KV_HEADS = N_HEADS // 4
GQA_GROUP = N_HEADS // N_KV_HEADS
Q_DIM = N_HEADS * HEAD_DIM
KV_DIM = N_KV_HEADS * HEAD_DIM
QKV_DIM = Q_DIM + 2 * KV_DIM
Q_BLOCK = 128
ROPE_THETA = 10000.0
ROPE_AXIS_DIM = HEAD_DIM // 2

D_INNER = 2 * D_MODEL
SSM_HEAD_DIM = 64
N_SSM_HEADS = D_INNER // SSM_HEAD_DIM
N_SSM_GROUPS = 8
HEADS_PER_GROUP = N_SSM_HEADS // N_SSM_GROUPS
D_STATE = 128
D_CONV = 5
CHUNK = 128
GN = N_SSM_GROUPS * D_STATE
CONV_DIM = D_INNER + 2 * GN
SSD_IN_DIM = D_INNER + CONV_DIM + 2 * N_SSM_HEADS

D_FF = 2816
FFN_CONV = 3

kernel_name = "hybrid_gqa_ssd_prefix_dit_block"


def rms_norm(x, g):
    xf = x.astype(jnp.float32)
    y = xf * lax.rsqrt(jnp.mean(xf * xf, axis=-1, keepdims=True) + EPS)
    return (y * g.astype(jnp.float32)).astype(x.dtype)


def modulate(x, shift, scale):
    return x * (1 + scale) + shift


def dwconv_centred(x, w, b):
    k_w = w.shape[0]
    pad = k_w // 2
    length = x.shape[1]
    xp = jnp.pad(x, ((0, 0), (pad, pad), (0, 0)))
    out = b + xp[:, 0:length] * w[0]
    for k in range(1, k_w):
        out = out + xp[:, k:k + length] * w[k]
    return out


def axial_rope_tables(n_tokens):
    rows = n_tokens // GRID_W
    t = jnp.arange(rows * GRID_W)
    row = (t // GRID_W).astype(jnp.float32)
    col = (t % GRID_W).astype(jnp.float32)
    inv = ROPE_THETA ** (-jnp.arange(0, ROPE_AXIS_DIM, 2, dtype=jnp.float32) / ROPE_AXIS_DIM)
    ang = jnp.concatenate([row[:, None] * inv, col[:, None] * inv], axis=-1)
    return jnp.cos(ang), jnp.sin(ang)


def apply_rope(x, cos, sin):
    xr = x.astype(jnp.float32).reshape(*x.shape[:-1], HEAD_DIM // 2, 2)
    x1, x2 = xr[..., 0], xr[..., 1]
    c = cos[None, :, None, :]
    s = sin[None, :, None, :]
    out = jnp.stack([x1 * c - x2 * s, x1 * s + x2 * c], axis=-1)
    return out.reshape(x.shape).astype(x.dtype)


def gqa_attend(q, k, v):
    s = jnp.einsum('bqhgd,bkhd->bhgqk', q, k).astype(jnp.float32) * (HEAD_DIM ** -0.5)
    p = jax.nn.softmax(s, axis=-1).astype(v.dtype)
    return jnp.einsum('bhgqk,bkhd->bqhgd', p, v)


def attention_mixer(h_lat, h_ctx, w_qkv, q_norm, k_norm, w_o, need_ctx_out):
    bsz, seq, _ = h_lat.shape
    l_ctx = h_ctx.shape[1]

    q_l, k_l, v_l = jnp.split(h_lat @ w_qkv, [Q_DIM, Q_DIM + KV_DIM], axis=-1)
    q_l = rms_norm(q_l.reshape(bsz, seq, N_HEADS, HEAD_DIM), q_norm)
    k_l = rms_norm(k_l.reshape(bsz, seq, N_KV_HEADS, HEAD_DIM), k_norm)
    v_l = v_l.reshape(bsz, seq, N_KV_HEADS, HEAD_DIM)
    cos, sin = axial_rope_tables(seq)
    q_l = apply_rope(q_l, cos, sin)
    k_l = apply_rope(k_l, cos, sin)

    if need_ctx_out:
        q_c, k_c, v_c = jnp.split(h_ctx @ w_qkv, [Q_DIM, Q_DIM + KV_DIM], axis=-1)
    else:
        k_c, v_c = jnp.split(h_ctx @ w_qkv[:, Q_DIM:], [KV_DIM], axis=-1)
    k_c = rms_norm(k_c.reshape(bsz, l_ctx, N_KV_HEADS, HEAD_DIM), k_norm)
    v_c = v_c.reshape(bsz, l_ctx, N_KV_HEADS, HEAD_DIM)

    k_all = jnp.concatenate([k_l, k_c], axis=1)
    v_all = jnp.concatenate([v_l, v_c], axis=1)
    n_blk = seq // Q_BLOCK
    qb = q_l.reshape(bsz, n_blk, Q_BLOCK, N_KV_HEADS, GQA_GROUP, HEAD_DIM).transpose(1, 0, 2, 3, 4, 5)
    ob = lax.map(lambda qblk: gqa_attend(qblk, k_all, v_all), qb)
    o_l = ob.transpose(1, 0, 2, 3, 4, 5).reshape(bsz, seq, Q_DIM) @ w_o

    o_c = None
    if need_ctx_out:
        q_c = rms_norm(q_c.reshape(bsz, l_ctx, N_HEADS, HEAD_DIM), q_norm)
        q_c = q_c.reshape(bsz, l_ctx, N_KV_HEADS, GQA_GROUP, HEAD_DIM)
        o_c = gqa_attend(q_c, k_c, v_c).reshape(bsz, l_ctx, Q_DIM) @ w_o
    return o_l, o_c


def ssd_scan(x, dt, a, bm, cm, h0, return_y):
    bsz, length = x.shape[:2]
    nc = length // CHUNK
    xc = x.reshape(bsz, nc, CHUNK, N_SSM_GROUPS, HEADS_PER_GROUP, SSM_HEAD_DIM)
    dtc = dt.reshape(bsz, nc, CHUNK, N_SSM_GROUPS, HEADS_PER_GROUP)
    bc = bm.reshape(bsz, nc, CHUNK, N_SSM_GROUPS, D_STATE)
    cc = cm.reshape(bsz, nc, CHUNK, N_SSM_GROUPS, D_STATE)
    acs = jnp.cumsum(dtc * a, axis=2)

    decay_end = jnp.exp(acs[:, :, -1:] - acs)
    xw = xc * (decay_end * dtc)[..., None]
    states = jnp.einsum('bcsgn,bcsgrp->bcgrpn', bc, xw)
    chunk_decay = jnp.exp(acs[:, :, -1])

    def step(h, inp):
        st, dec = inp
        return h * dec[..., None, None] + st, h

    h_final, h_in = lax.scan(step, h0, (jnp.moveaxis(states, 1, 0), jnp.moveaxis(chunk_decay, 1, 0)))
    if not return_y:
        return None, h_final
    h_in = jnp.moveaxis(h_in, 0, 1)

    seg = acs[:, :, :, None] - acs[:, :, None, :]
    mask = jnp.tril(jnp.ones((CHUNK, CHUNK), dtype=bool))[:, :, None, None]
    decay = jnp.exp(jnp.where(mask, seg, -jnp.inf))
    cb = jnp.einsum('bcqgn,bcsgn->bcqsg', cc, bc)
    w = cb[..., None] * decay * dtc[:, :, None]
    y_diag = jnp.einsum('bcqsgr,bcsgrp->bcqgrp', w, xc)
    y_off = jnp.einsum('bcqgn,bcgrpn->bcqgrp', cc, h_in) * jnp.exp(acs)[..., None]
    y = (y_diag + y_off).reshape(bsz, length, N_SSM_GROUPS, HEADS_PER_GROUP, SSM_HEAD_DIM)
    return y, h_final


def mamba2_mixer(h_lat, h_ctx, w_in, conv_w, conv_b, dt_bias_f, dt_bias_b, a_log_f, a_log_b,
                 d_skip, norm_w, w_out, need_ctx_out):
    a_f = -jnp.exp(a_log_f)
    a_b = -jnp.exp(a_log_b)
    flip = lambda t: jnp.flip(t, axis=1)

    def front(h):
        bsz, length, _ = h.shape
        z, xbc, dt = jnp.split(h @ w_in, [D_INNER, D_INNER + CONV_DIM], axis=-1)
        xbc = jax.nn.silu(dwconv_centred(xbc, conv_w, conv_b))
        xs, bm, cm = jnp.split(xbc, [D_INNER, D_INNER + GN], axis=-1)
        xs = xs.reshape(bsz, length, N_SSM_GROUPS, HEADS_PER_GROUP, SSM_HEAD_DIM)
        bm = bm.reshape(bsz, length, N_SSM_GROUPS, D_STATE)
        cm = cm.reshape(bsz, length, N_SSM_GROUPS, D_STATE)
        dt_f, dt_b = jnp.split(dt, 2, axis=-1)
        dt_f = jax.nn.softplus(dt_f.reshape(bsz, length, N_SSM_GROUPS, HEADS_PER_GROUP) + dt_bias_f)
        dt_b = jax.nn.softplus(dt_b.reshape(bsz, length, N_SSM_GROUPS, HEADS_PER_GROUP) + dt_bias_b)
        return z, xs, bm, cm, dt_f, dt_b

    def bidir(parts, h0f, h0b, return_y):
        z, xs, bm, cm, dt_f, dt_b = parts
        yf, hf = ssd_scan(xs, dt_f, a_f, bm, cm, h0f, return_y)
        yb, hb = ssd_scan(flip(xs), flip(dt_b), a_b, flip(bm), flip(cm), h0b, return_y)
        if not return_y:
            return None, hf, hb
        bsz, length = xs.shape[:2]
        y = yf + flip(yb) + d_skip[..., None] * xs
        y = rms_norm(y.reshape(bsz, length, D_INNER) * jax.nn.silu(z), norm_w) @ w_out
        return y, hf, hb

    parts_c = front(h_ctx)
    h0 = jnp.zeros((h_ctx.shape[0], N_SSM_GROUPS, HEADS_PER_GROUP, SSM_HEAD_DIM, D_STATE), parts_c[1].dtype)
    o_c, hf_c, hb_c = bidir(parts_c, h0, h0, need_ctx_out)
    o_l, _, _ = bidir(front(h_lat), hf_c, hb_c, True)
    return o_l, o_c


def conv_ffn(h, w_up, conv_w, conv_b, w_down):
    g, u = jnp.split(h @ w_up, 2, axis=-1)
    g = dwconv_centred(g, conv_w, conv_b)
    return (jax.nn.silu(g) * u) @ w_down


def setup_inputs(seed: int = 0) -> dict:
    key = jax.random.key(seed)
    ks = jax.random.split(key, 32)
    f32 = jnp.float32
    nrm = lambda k, shape, scale: jax.random.normal(k, shape, f32) * scale
    na, ns = N_ATTN_LAYERS, N_SSD_LAYERS
    G, R = N_SSM_GROUPS, HEADS_PER_GROUP

    dt_init = lambda k: jnp.exp(jax.random.uniform(k, (ns, G, R), f32, np.log(0.001), np.log(0.1)))
    dtf0 = dt_init(ks[14])
    dtb0 = dt_init(ks[15])
    return {
        "x": nrm(ks[0], (BATCH, SEQ, D_MODEL), 1.0),
        "c": nrm(ks[1], (BATCH, D_MODEL), 1.0),
        "ctx": nrm(ks[2], (BATCH, CTX_LEN, D_MODEL), 1.0),
        "c_ctx": nrm(ks[3], (D_MODEL,), 1.0),
        "ada_w": nrm(ks[4], (DEPTH, D_MODEL, 6 * D_MODEL), 0.5 * D_MODEL ** -0.5),
        "ada_b": nrm(ks[5], (DEPTH, 6 * D_MODEL), 0.02),
        "norm_mix_w": 1.0 + nrm(ks[6], (DEPTH, D_MODEL), 0.02),
        "norm_ffn_w": 1.0 + nrm(ks[7], (DEPTH, D_MODEL), 0.02),
        "attn_w_qkv": nrm(ks[8], (na, D_MODEL, QKV_DIM), D_MODEL ** -0.5),
        "attn_q_norm": 1.0 + nrm(ks[9], (na, HEAD_DIM), 0.02),
        "attn_k_norm": 1.0 + nrm(ks[10], (na, HEAD_DIM), 0.02),
        "attn_w_o": nrm(ks[11], (na, Q_DIM, D_MODEL), Q_DIM ** -0.5),
        "ssd_w_in": nrm(ks[12], (ns, D_MODEL, SSD_IN_DIM), D_MODEL ** -0.5),
        "ssd_conv_w": nrm(ks[13], (ns, D_CONV, CONV_DIM), D_CONV ** -0.5),
        "ssd_conv_b": nrm(ks[16], (ns, CONV_DIM), 0.02),
        "ssd_dt_bias_f": dtf0 + jnp.log(-jnp.expm1(-dtf0)),
        "ssd_dt_bias_b": dtb0 + jnp.log(-jnp.expm1(-dtb0)),
        "ssd_a_log_f": jnp.log(jax.random.uniform(ks[17], (ns, G, R), f32, 1.0, 16.0)),
        "ssd_a_log_b": jnp.log(jax.random.uniform(ks[18], (ns, G, R), f32, 1.0, 16.0)),
        "ssd_d": 1.0 + nrm(ks[19], (ns, G, R), 0.02),
        "ssd_norm_w": 1.0 + nrm(ks[20], (ns, D_INNER), 0.02),
        "ssd_w_out": nrm(ks[21], (ns, D_INNER, D_MODEL), D_INNER ** -0.5),
        "ffn_w_up": nrm(ks[22], (DEPTH, D_MODEL, 2 * D_FF), D_MODEL ** -0.5),
        "ffn_conv_w": nrm(ks[23], (DEPTH, FFN_CONV, D_FF), FFN_CONV ** -0.5),
        "ffn_conv_b": nrm(ks[24], (DEPTH, D_FF), 0.02),
        "ffn_w_down": nrm(ks[25], (DEPTH, D_FF, D_MODEL), D_FF ** -0.5),
    }


def reference(x, c, ctx, c_ctx, ada_w, ada_b, norm_mix_w, norm_ffn_w,
              attn_w_qkv, attn_q_norm, attn_k_norm, attn_w_o,
              ssd_w_in, ssd_conv_w, ssd_conv_b, ssd_dt_bias_f, ssd_dt_bias_b,
              ssd_a_log_f, ssd_a_log_b, ssd_d, ssd_norm_w, ssd_w_out,
              ffn_w_up, ffn_conv_w, ffn_conv_b, ffn_w_down):
    silu_c = jax.nn.silu(c)
    silu_cc = jax.nn.silu(c_ctx)
    for i in range(DEPTH):
        need_ctx_out = i < DEPTH - 1
        mod_l = jnp.split((silu_c @ ada_w[i] + ada_b[i])[:, None, :], 6, axis=-1)
        mod_c = jnp.split(silu_cc @ ada_w[i] + ada_b[i], 6, axis=-1)

        h_l = modulate(rms_norm(x, norm_mix_w[i]), mod_l[0], mod_l[1])
        h_c = modulate(rms_norm(ctx, norm_mix_w[i]), mod_c[0], mod_c[1])
        j = i // N_MIXERS
        if i % N_MIXERS == 0:
            o_l, o_c = attention_mixer(h_l, h_c, attn_w_qkv[j], attn_q_norm[j], attn_k_norm[j],
                                       attn_w_o[j], need_ctx_out)
        else:
            o_l, o_c = mamba2_mixer(h_l, h_c, ssd_w_in[j], ssd_conv_w[j], ssd_conv_b[j],
                                    ssd_dt_bias_f[j], ssd_dt_bias_b[j], ssd_a_log_f[j], ssd_a_log_b[j],
                                    ssd_d[j], ssd_norm_w[j], ssd_w_out[j], need_ctx_out)
        x = x + mod_l[2] * o_l
        if need_ctx_out:
            ctx = ctx + mod_c[2] * o_c

        h_l = modulate(rms_norm(x, norm_ffn_w[i]), mod_l[3], mod_l[4])
        x = x + mod_l[5] * conv_ffn(h_l, ffn_w_up[i], ffn_conv_w[i], ffn_conv_b[i], ffn_w_down[i])
        if need_ctx_out:
            h_c = modulate(rms_norm(ctx, norm_ffn_w[i]), mod_c[3], mod_c[4])
            ctx = ctx + mod_c[5] * conv_ffn(h_c, ffn_w_up[i], ffn_conv_w[i], ffn_conv_b[i], ffn_w_down[i])
    return x
```

```python
import numpy as np
from contextlib import ExitStack
import concourse.bass as bass
import concourse.mybir as mybir
from concourse.bass_utils import run_bass_kernel_spmd

F32 = mybir.dt.float32
BF16 = mybir.dt.bfloat16
AF = mybir.ActivationFunctionType
ALU = mybir.AluOpType
AX = mybir.AxisListType

D = 1024
L = 2048
LC = 256
T = L + LC
KC = 8
EPS = 1e-6
NH, HD = 16, 64
DFF = 2816
NFC = 22
PAIRS = [(0, 4), (1, 5), (2, 6), (3, 7), (8, 12), (9, 13), (10, 14), (11, 15)]
DI = 2048
NG = 8
DST = 128
NCH = L // 128
SB_BASE = 16512
SB_END = 229376
TCH = [(0, 512), (512, 512), (1024, 512), (1536, 512), (2048, 256)]


class Sched:
    def __init__(self, nc, stack, ndma=24):
        self.nc = nc
        self.eng = {}
        for name in ("pe", "act", "dve", "pool", "sp"):
            sem = stack.enter_context(nc.semaphore("s_" + name))
            self.eng[name] = dict(sem=sem, count=0, seen={}, prog=[])
        self.dma_ch = []
        for i in range(ndma):
            sem = stack.enter_context(nc.semaphore("s_dma%d" % i))
            self.dma_ch.append(dict(sem=sem, count=0))
        self.dma_rrs = [0, 0]
        self.nwait = 0

    def tile(self, name, after=None):
        return dict(name=name, w=None, r=dict(after) if after else {})

    def collect(self, tiles):
        ev = {}
        for t in tiles:
            if t["w"] is not None:
                s, v = t["w"]
                if ev.get(s.num, (s, 0))[1] < v:
                    ev[s.num] = (s, v)
            for k, (s, v) in t["r"].items():
                if ev.get(k, (s, 0))[1] < v:
                    ev[k] = (s, v)
        return ev

    def fence(self, tiles, ev):
        for t in tiles:
            for k, (s, v) in ev.items():
                if t["r"].get(k, (s, 0))[1] < v:
                    t["r"][k] = (s, v)

    def _wait(self, e, sem, val):
        E = self.eng[e]
        if E["seen"].get(sem.num, 0) >= val:
            return
        E["seen"][sem.num] = val
        E["prog"].append(("wait", sem, val))
        self.nwait += 1

    def _deps(self, e, reads, writes, own_ok=False, nowaw=False):
        need = {}
        own_ = self.eng[e]["sem"].num
        for t in reads:
            if t["w"] is not None:
                s, v = t["w"]
                if need.get(s.num, (s, 0))[1] < v:
                    need[s.num] = (s, v)
            if t.get("excl"):
                for k, (s, v) in t["r"].items():
                    if k != own_ and need.get(k, (s, 0))[1] < v:
                        need[k] = (s, v)
        for t in writes:
            if t["w"] is not None:
                s, v = t["w"]
                if not (nowaw and s.num == own_) and need.get(s.num, (s, 0))[1] < v:
                    need[s.num] = (s, v)
            for k, (s, v) in t["r"].items():
                if need.get(k, (s, 0))[1] < v:
                    need[k] = (s, v)
        own = self.eng[e]["sem"].num
        for k, (s, v) in need.items():
            if own_ok and k == own:
                continue
            self._wait(e, s, v)

    def _mark(self, ev, reads, writes):
        s, v = ev
        for t in reads:
            t["r"][s.num] = (s, v)
        for t in writes:
            t["w"] = ev
            t["r"] = {}

    def op(self, e, reads, writes, fn, own_ok=False, nowaw=False):
        E = self.eng[e]
        self._deps(e, reads, writes, own_ok, nowaw)
        E["count"] += 1
        E["prog"].append(("op", fn, E["sem"], 1))
        self._mark((E["sem"], E["count"]), reads, writes)

    def dma(self, e, reads, writes, fn):
        E = self.eng[e]
        half = len(self.dma_ch) // 2
        k = 0 if e == "pool" else 1
        ch = self.dma_ch[k * half + self.dma_rrs[k]]
        self.dma_rrs[k] = (self.dma_rrs[k] + 1) % half
        if ch["count"] > 0:
            self._wait(e, ch["sem"], ch["count"])
        self._deps(e, reads, writes)
        ch["count"] += 16
        E["prog"].append(("op", fn, ch["sem"], 16))
        self._mark((ch["sem"], ch["count"]), reads, writes)

    def wait_tiles(self, e, tiles):
        self._deps(e, tiles, tiles)

    def emit(self, block):
        nc = self.nc

        def run(E, h):
            def body(eng):
                for it in E["prog"]:
                    if it[0] == "wait":
                        h.wait_ge(it[1], it[2])
                    else:
                        it[1](h).then_inc(it[2], it[3])
            return body
        block.tensor(run(self.eng["pe"], nc.tensor))
        block.scalar(run(self.eng["act"], nc.scalar))
        block.vector(run(self.eng["dve"], nc.vector))
        block.gpsimd(run(self.eng["pool"], nc.gpsimd))
        block.sync(run(self.eng["sp"], nc.sync))


class Buf:
    def __init__(self, t, tiles):
        self.t = t
        self.tl = tiles

    def __getitem__(self, idx):
        return self.t[idx]


class Ring:
    def __init__(self, bufs):
        self.bufs = bufs
        self.i = 0

    def next(self):
        b = self.bufs[self.i % len(self.bufs)]
        self.i += 1
        return b


class Ctx:
    pass


def build_program(n_layers=2, dbg=False):
    nc = bass.Bass("TRN2", target_bir_lowering=False)
    dt = nc.dram_tensor

    def din(name, shape, dtype=F32):
        return dt(name, list(shape), dtype, kind="ExternalInput").ap()

    d_xT = din("xT", [128, KC, L])
    d_ctxT = din("ctxT", [128, KC, LC])
    d_cc = din("cc", [128, KC, 2])
    d_adaw = din("ada_w", [2, 12, 128, KC, 512])
    d_adab = din("ada_b", [128, 2, 48])
    d_nw = din("nw", [128, 2, 2, KC])
    d_wq = din("wq", [8, 128, KC, 128])
    d_wk = din("wk", [2, 128, KC, 128])
    d_wv = din("wv", [128, KC, 256])
    d_qkn = din("qkn", [128, 2])
    d_wo = din("wo", [8, 128, 8, 128])
    d_cs = din("cs", [128, 2, L])
    d_cst = din("cst", [128, 9, 128])
    d_cstf = din("cstf", [128, 5, 128])
    d_wup = din("wup", [2, NFC, 128, KC, 256])
    d_wdn = din("wdn", [2, NFC, 128, D])
    d_fcw = din("fcw", [128, 2, NFC, 4])
    d_win = din("win", [NG, 6, 128, KC, 128])
    d_scw = din("scw", [128, NG, 4, 6])
    d_wdt = din("wdt", [128, KC, 64])
    d_dtb = din("dtb", [128, 64])
    d_alg = din("alg", [128, 64])
    d_sdd = din("sdd", [128, NG, 2])
    d_snw = din("snw16", [128, 16])
    d_wout = din("wout", [8, 128, 16, 128])
    d_out = dt("outT", [128, KC, L], F32, kind="ExternalOutput").ap()
    d_xs = dt("x_spill", [128, KC, L], F32, kind="Internal").ap()
    d_vs = dt("v_spill", [128, 16, L], BF16, kind="Internal").ap()

    st = ExitStack()
    with st:
        S = Sched(nc, st)
        cur = [SB_BASE]

        def sb_at(name, shape, dtype, off):
            return nc.alloc_sbuf_tensor_at(name, list(shape), dtype, offset=off)

        def nbytes(shape, dtype):
            n = 1
            for s in shape[1:]:
                n *= s
            return n * (4 if dtype == F32 else 2)

        def sb(name, shape, dtype, ntiles=1):
            off = cur[0]
            sz = (nbytes(shape, dtype) + 63) // 64 * 64
            cur[0] += sz
            assert cur[0] <= SB_END, (name, cur[0])
            t = sb_at(name, shape, dtype, off)
            return Buf(t, [S.tile("%s.%d" % (name, i)) for i in range(ntiles)])

        class Region:
            def __init__(self, base, size):
                self.base, self.size, self.cur = base, size, base
                self.tiles = []

            def reset(self, extra_tiles=()):
                ev = S.collect(self.tiles + list(extra_tiles))
                self.cur = self.base
                self.tiles = []
                self.after = ev
                return ev

            def sb(self, name, shape, dtype, ntiles=1):
                sz = (nbytes(shape, dtype) + 63) // 64 * 64
                off = self.cur
                self.cur += sz
                assert self.cur <= self.base + self.size, (name, self.cur - self.base, self.size)
                t = sb_at(name, shape, dtype, off)
                tiles = [S.tile("%s.%d" % (name, i), after=getattr(self, "after", None)) for i in range(ntiles)]
                self.tiles += tiles
                return Buf(t, tiles)

        X = sb("X", [128, KC, L], F32, ntiles=4)
        xreg_base = SB_BASE
        cx_base = cur[0]
        CX = sb("CX", [128, KC, LC], F32)
        h_base = cur[0]
        H = sb("H", [128, KC, T], BF16, ntiles=5)
        MOD = sb("MOD", [128, 2, 48, 2], F32, ntiles=2)
        ADAB = sb("ADAB", [128, 2, 48], F32)
        NW = sb("NW", [128, 2, 2, KC], F32)
        SS = sb("SS", [128, 4, 2, 2, KC], F32, ntiles=4)
        CC = sb("CC", [128, KC, 2], F32)
        SCC = sb("SCC", [128, KC, 2], BF16)
        CST = sb("CST", [128, 9, 128], BF16)
        CSTF = sb("CSTF", [128, 5, 128], F32)
        QKN = sb("QKN", [128, 2], F32)
        FCW = sb("FCW", [128, 2, NFC, 4], F32)
        SQR = Ring([sb("SQ%d" % i, [128, 512], BF16) for i in range(3)])
        RSR = Ring([sb("RS%d" % i, [128, 512], F32) for i in range(2)])
        TMR = Ring([sb("TM%d" % i, [128, 512], F32) for i in range(3)])
        EPSB = sb("EPSB", [128, 2], F32)
        persist_end = cur[0]
        PH = Region(persist_end, SB_END - persist_end)
        XR = Region(xreg_base, KC * L * 4)
        XR.tiles = list(X.tl)

        class PView(Buf):
            def __init__(self, t, off, tiles):
                self.t, self.off, self.tl = t, off, tiles

            def __getitem__(self, idx):
                ps, cs = idx
                a = self.off + (cs.start or 0)
                b_ = self.off + (cs.stop if cs.stop is not None else 512)
                return self.t[ps, a:b_]

        banks = []
        dbanks = []
        for i in range(4):
            t = st.enter_context(nc.psum_tensor("pd%d" % i, [128, 1024], F32))
            tl_ = []
            for h_ in range(2):
                tile_ = S.tile("pb%d" % (2 * i + h_))
                tile_["excl"] = True
                banks.append(PView(t, h_ * 512, [tile_]))
                tl_.append(tile_)
            dbanks.append(Buf(t, tl_))
        BK = Ring(banks[:6])
        BKO = Ring(banks[6:])

        onesF = CST[:, 0, :]
        onesBD = CST[:, 1, :]
        rotM = CST[:, 2, :]

        dumps = []

        def dump(name, buf_ap, tiles, shape, dtype):
            if not dbg:
                return
            d_ = dt("dbg_" + name, list(shape), dtype, kind="ExternalOutput").ap()
            dumps.append("dbg_" + name)
            S.dma("sp", tiles, [], lambda e: e.dma_start(out=d_, in_=buf_ap))

        def mm(out_ap, lhsT_ap, rhs_ap, start, stop, reads, wtile, sgc=False):
            if sgc:
                S.op("pe", reads, [wtile],
                     lambda e: e.matmul(out_ap, lhsT_ap, rhs_ap, start=start, stop=stop, skip_group_check=True),
                     own_ok=True)
            else:
                S.op("pe", reads, [wtile],
                     lambda e: e.matmul(out_ap, lhsT_ap, rhs_ap, start=start, stop=stop), own_ok=True)

        def load(eng, buf, dst_ap, src_ap, tiles=None):
            S.dma(eng, [], tiles if tiles is not None else buf.tl,
                  lambda e: e.dma_start(out=dst_ap, in_=src_ap))

        def act(out_ap, in_ap, func, reads, writes, bias=None, scale=None, eng="act", nowaw=False):
            kw = {}
            if bias is not None:
                kw["bias"] = bias
            if scale is not None:
                kw["scale"] = scale
            S.op("act", reads, writes, lambda e: e.activation(out=out_ap, in_=in_ap, func=func, **kw), nowaw=nowaw)

        def tt(eng, out_ap, in0, in1, op, reads, writes, nowaw=False):
            S.op(eng, reads, writes, lambda e: e.tensor_tensor(out=out_ap, in0=in0, in1=in1, op=op), nowaw=nowaw)

        def ts(eng, out_ap, in0, s1, s2, op0, op1, reads, writes):
            if s2 is None:
                S.op(eng, reads, writes,
                     lambda e: e.tensor_scalar(out=out_ap, in0=in0, scalar1=s1, scalar2=None, op0=op0))
            else:
                S.op(eng, reads, writes,
                     lambda e: e.tensor_scalar(out=out_ap, in0=in0, scalar1=s1, scalar2=s2, op0=op0, op1=op1))

        def stt(out_ap, in0, scalar, in1, op0, op1, reads, writes, nowaw=False):
            S.op("dve", reads, writes,
                 lambda e: e.scalar_tensor_tensor(out=out_ap, in0=in0, scalar=scalar, in1=in1, op0=op0, op1=op1),
                 nowaw=nowaw)

        def rstd_from_ms(ps, n):
            rs = RSR.next()
            act(rs[:, :n], ps[:, :n], AF.Ln, [ps.tl[0]] + EPSB.tl, rs.tl, bias=EPSB[:, 0:1])
            act(rs[:, :n], rs[:, :n], AF.Exp, rs.tl, rs.tl, scale=-0.5)
            return rs

        S.op("dve", [], EPSB.tl, lambda e: e.memset(EPSB[:, 0:1], EPS))
        S.op("dve", [], EPSB.tl, lambda e: e.memset(EPSB[:, 1:2], 1.0))
        load("sp", CC, CC[:], d_cc)
        load("sp", ADAB, ADAB[:], d_adab)
        load("sp", NW, NW[:], d_nw)
        load("sp", QKN, QKN[:], d_qkn)
        load("sp", FCW, FCW[:], d_fcw)
        load("pool", CST, CST[:], d_cst)
        load("sp", CSTF, CSTF[:], d_cstf)
        for tc in range(4):
            t0, n = TCH[tc]
            load("sp", X, X[:, :, t0:t0 + n], d_xT[:, :, t0:t0 + n], tiles=[X.tl[tc]])
        load("sp", CX, CX[:], d_ctxT)
        act(SCC[:], CC[:], AF.Silu, CC.tl, SCC.tl)

        def ada_piece(layer, piece, slab):
            pb = BK.next()
            for j in range(4):
                for kc in range(KC):
                    mm(pb[:, 2 * j:2 * j + 2], slab[:, kc, j * 128:(j + 1) * 128], SCC[:, kc, :],
                       kc == 0, kc == KC - 1, slab.tl + SCC.tl, pb.tl[0])
            o = MOD[:, layer, piece * 4:(piece + 1) * 4, :]
            b = ADAB[:, layer, piece * 4:(piece + 1) * 4]
            tt("dve", o, pb[:, 0:8].rearrange("p (a b) -> p a b", b=2), b.unsqueeze(2).to_broadcast([128, 4, 2]),
               ALU.add, pb.tl + ADAB.tl, [MOD.tl[layer]])

        def ada_load(layer, piece, slab):
            load("pool", slab, slab[:], d_adaw[layer, piece])

        def make_ss(layer, sub):
            si = layer * 2 + sub
            for lc in range(2):
                sc = MOD[:, layer, (sub * 3 + 1) * 8:(sub * 3 + 2) * 8, lc]
                sh = MOD[:, layer, (sub * 3 + 0) * 8:(sub * 3 + 1) * 8, lc]
                stt(SS[:, si, lc, 0, :], sc, 1.0, NW[:, sub, layer, :], ALU.add, ALU.mult,
                    [MOD.tl[layer]] + NW.tl, [SS.tl[si]])
                S.op("dve", [MOD.tl[layer]], [SS.tl[si]],
                     lambda e, sh=sh, si=si, lc=lc: e.tensor_copy(out=SS[:, si, lc, 1, :], in_=sh))

        def gate_ap(layer, sub, lc, oc):
            return MOD[:, layer, (sub * 3 + 2) * 8 + oc, lc:lc + 1]

        def norm_mod(layer, sub, chunks):
            si = layer * 2 + sub
            for tc in chunks:
                t0, n = TCH[tc]
                lc = 1 if tc == 4 else 0
                src = (lambda kc: CX[:, kc, :]) if tc == 4 else (lambda kc: X[:, kc, t0:t0 + n])
                stile = CX.tl[0] if tc == 4 else X.tl[tc]
                pb = BK.next()
                for kc in range(KC):
                    sq = SQR.next()
                    act(sq[:, :n], src(kc), AF.Square, [stile], sq.tl)
                    mm(pb[:, :n], onesF, sq[:, :n], kc == 0, kc == KC - 1, sq.tl + CST.tl, pb.tl[0])
                rs = rstd_from_ms(pb, n)
                for kc in range(KC):
                    tm = TMR.next()
                    stt(tm[:, :n], src(kc), SS[:, si, lc, 0, kc:kc + 1], rs[:, :n], ALU.mult, ALU.mult,
                        [stile, SS.tl[si]] + rs.tl, tm.tl)
                    act(H[:, kc, t0:t0 + n], tm[:, :n], AF.Identity, tm.tl + [SS.tl[si]], [H.tl[tc]],
                        bias=SS[:, si, lc, 1, kc:kc + 1], nowaw=True)

        adas_base = [0]

        def attention_layer(layer, post_chunk=None):
            PH.reset()
            adas_base[0] = PH.cur
            ADAS = Ring([PH.sb("ADAS%d" % i, [128, KC, 512], BF16) for i in range(2)])
            slabs = {}
            slabs[0] = ADAS.next(); ada_load(0, 0, slabs[0])

            def ada_step(p):
                if p + 1 < 24:
                    slabs[p + 1] = ADAS.next(); ada_load((p + 1) // 12, (p + 1) % 12, slabs[p + 1])
                ada_piece(p // 12, p % 12, slabs.pop(p))
                if p == 3:
                    make_ss(0, 0)
                if p == 9:
                    make_ss(0, 1)
                if p == 23:
                    make_ss(1, 0)
                    make_ss(1, 1)
            for p in range(12):
                ada_step(p)
                if 3 <= p < 8:
                    norm_mod(layer, 0, [p - 3])
            dump("H", H[:], H.tl, [128, KC, T], BF16)
            dump("MOD", MOD[:], MOD.tl, [128, 2, 48, 2], F32)
            XR.reset()
            KT = XR.sb("KT", [128, 2, T], BF16, ntiles=10)
            VA = XR.sb("VA", [128, 18, 4, 128], BF16, ntiles=18)
            CS = XR.sb("CS", [128, 2, L], F32)
            QWR = Ring([XR.sb("QW%d" % i, [128, 512], F32) for i in range(4)])
            QBR = Ring([XR.sb("QB%d" % i, [128, 512], BF16) for i in range(3)])
            T1R = Ring([XR.sb("T1%d" % i, [128, 512], F32) for i in range(3)])
            T2R = Ring([XR.sb("T2%d" % i, [128, 512], F32) for i in range(2)])
            QT = PH.sb("QT", [128, 8, T], BF16, ntiles=80)
            WSL = Ring([PH.sb("WSL%d" % i, [128, KC, 128], BF16) for i in range(3)])
            WV = PH.sb("WV", [128, KC, 256], BF16)
            RCR = Ring([PH.sb("RC%d" % i, [128, 512], F32) for i in range(2)])
            load("sp", CS, CS[:], d_cs)
            load("pool", WV, WV[:], d_wv)
            S.op("dve", [], VA.tl, lambda e: e.memset(VA[:], 1.0))

            for blk in range(18):
                pb = BK.next()
                tc = min(blk // 4, 4)
                for kc in range(KC):
                    mm(pb[:, :256], H[:, kc, blk * 128:(blk + 1) * 128], WV[:, kc, :], kc == 0, kc == KC - 1,
                       [H.tl[tc]] + WV.tl, pb.tl[0])
                pv = pb[:, :256].rearrange("p (g d) -> p g d", d=64)
                S.op("act", pb.tl, [VA.tl[blk]],
                     lambda e, blk=blk, pv=pv: e.copy(out=VA[:, blk, 0:4:2, 0:64], in_=pv[:, 0:4:2, :]))
                S.op("dve", pb.tl, [VA.tl[blk]],
                     lambda e, blk=blk, pv=pv: e.tensor_copy(out=VA[:, blk, 1:4:2, 64:128], in_=pv[:, 1:4:2, :]))

            wsl = {}

            def wload(i):
                s_ = WSL.next()
                if i < 2:
                    load("pool", s_, s_[:], d_wk[i])
                else:
                    load("pool", s_, s_[:], d_wq[i - 2])
                wsl[i] = s_

            units = [(i, tc) for i in range(10) for tc in range(5)]
            ust = {}

            def unit_dst(i, tc):
                if i < 2:
                    return KT[:, i, :], [KT.tl[i * 5 + tc]], 1
                j = i - 2
                return QT[:, j, :], [QT.tl[(j * 2) * 5 + tc], QT.tl[(j * 2 + 1) * 5 + tc]], 0

            def pstage1(u):
                i, tc = units[u]
                if tc == 0 and i + 2 < 10:
                    wload(i + 2)
                slab = wsl[i]
                t0, n = TCH[tc]
                pb = BK.next()
                for kc in range(KC):
                    mm(pb[:, :n], slab[:, kc, :], H[:, kc, t0:t0 + n], kc == 0, kc == KC - 1,
                       slab.tl + [H.tl[tc]], pb.tl[0])
                sq = SQR.next()
                act(sq[:, :n], pb[:, :n], AF.Square, pb.tl, sq.tl)
                ust[u] = dict(pb=pb, sq=sq)

            def pstage2(u):
                i, tc = units[u]
                t0, n = TCH[tc]
                dst, dtl, wcol = unit_dst(i, tc)
                U = ust[u]
                pm = BK.next()
                mm(pm[:, :n], onesBD, U["sq"][:, :n], True, True, U["sq"].tl + CST.tl, pm.tl[0])
                rs = rstd_from_ms(pm, n)
                qw = QWR.next()
                stt(qw[:, :n], U["pb"][:, :n], QKN[:, wcol:wcol + 1], rs[:, :n], ALU.mult, ALU.mult,
                    U["pb"].tl + QKN.tl + rs.tl, qw.tl)
                U["qw"] = qw
                if tc == 4:
                    S.op("act", qw.tl, dtl,
                         lambda e, qw=qw, t0=t0, n=n, dst=dst: e.copy(out=dst[:, t0:t0 + n], in_=qw[:, :n]))
                    return

            def pstage3(u):
                i, tc = units[u]
                if tc == 4:
                    return
                t0, n = TCH[tc]
                U = ust[u]
                qw = U["qw"]
                qb = QBR.next()
                S.op("act", qw.tl, qb.tl, lambda e, qw=qw, qb=qb, n=n: e.copy(out=qb[:, :n], in_=qw[:, :n]))
                U["qb"] = qb
                t1 = T1R.next()
                tt("pool", t1[:, :n], qw[:, :n], CS[:, 0, t0:t0 + n], ALU.mult, qw.tl + CS.tl, t1.tl)
                U["t1"] = t1

            def pstage4(u):
                i, tc = units[u]
                U = ust.pop(u)
                if tc == 4:
                    return
                t0, n = TCH[tc]
                dst, dtl, wcol = unit_dst(i, tc)
                qb, t1 = U["qb"], U["t1"]
                pr = BK.next()
                mm(pr[:, :n], rotM, qb[:, :n], True, True, qb.tl + CST.tl, pr.tl[0])
                t2 = T2R.next()
                tt("dve", t2[:, :n], pr[:, :n], CS[:, 1, t0:t0 + n], ALU.mult, pr.tl + CS.tl, t2.tl)
                tt("dve", dst[:, t0:t0 + n], t1[:, :n], t2[:, :n], ALU.add, t1.tl + t2.tl, dtl)

            wload(0)
            wload(1)
            NU = len(units)
            for n_ in range(NU + 3):
                if n_ % 4 == 0 and n_ // 4 < 12:
                    ada_step(12 + n_ // 4)
                if n_ < NU:
                    pstage1(n_)
                if 0 <= n_ - 1 < NU:
                    pstage2(n_ - 1)
                if 0 <= n_ - 2 < NU:
                    pstage3(n_ - 2)
                if 0 <= n_ - 3 < NU:
                    pstage4(n_ - 3)

            dump("QT", QT[:], QT.tl, [128, 8, T], BF16)
            dump("KT", KT[:], KT.tl, [128, 2, T], BF16)
            dump("VA", VA[:], VA.tl, [128, 18, 4, 128], BF16)
            SDR = Ring(dbanks[0:2])
            ODR = Ring(dbanks[2:4])
            PT2R = Ring([PH.sb("PTT%d" % i, [128, 2, 512], BF16) for i in range(4)])
            LAG = 1
            steps = []
            for j in range(8):
                for tc in range(5):
                    kbs = list(range(18)) if tc < 4 else [16, 17]
                    for i, kb in enumerate(kbs):
                        steps.append((j, tc, kb, i == 0, i == len(kbs) - 1))
            pend = []
            ostate = {}

            def emit_pv(item):
                (j, tc, kb, first, last, pt) = item
                t0, n = TCH[tc]
                if first:
                    ostate[(j, tc)] = ODR.next()
                od = ostate[(j, tc)]
                for hf in range(2):
                    g = PAIRS[j][hf] // 4
                    mm(od[:, hf * 512:hf * 512 + n], VA[:, kb, g, :], pt[:, hf, :n], first, last,
                       [VA.tl[kb]] + pt.tl, od.tl[hf])
                if last:
                    for hf in range(2):
                        lo, dn = (0, 64) if hf == 0 else (64, 0)
                        c0 = hf * 512
                        rc = RCR.next()
                        S.op("dve", [od.tl[hf]], rc.tl,
                             lambda e, rc=rc, od=od, dn=dn, n=n, c0=c0: e.reciprocal(
                                 out=rc[dn:dn + 64, :n], in_=od[dn:dn + 64, c0:c0 + n]))
                        tt("dve", QT[lo:lo + 64, j, t0:t0 + n], od[lo:lo + 64, c0:c0 + n], rc[dn:dn + 64, :n], ALU.mult,
                           [od.tl[hf]] + rc.tl, [QT.tl[(j * 2 + hf) * 5 + tc]])

            for (j, tc, kb, first, last) in steps:
                t0, n = TCH[tc]
                kp = (PAIRS[j][0] // 4) // 2
                ktc = min(kb // 4, 4)
                sd = SDR.next()
                for hf in range(2):
                    mm(sd[:, hf * 512:hf * 512 + n], KT[hf * 64:hf * 64 + 64, kp, kb * 128:(kb + 1) * 128],
                       QT[hf * 64:hf * 64 + 64, j, t0:t0 + n], True, True,
                       [KT.tl[kp * 5 + ktc], QT.tl[(j * 2 + hf) * 5 + tc]], sd.tl[hf])
                pt = PT2R.next()
                act(pt[:, :, :n], sd[:, :].rearrange("p (h q) -> p h q", q=512)[:, :, :n], AF.Exp, sd.tl, pt.tl, scale=0.125)
                pend.append((j, tc, kb, first, last, pt))
                if len(pend) > LAG:
                    emit_pv(pend.pop(0))
            while pend:
                emit_pv(pend.pop(0))

            dump("OT", QT[:], QT.tl, [128, 8, T], BF16)
            ev = XR.reset()
            S.fence(X.tl, ev)
            XR.tiles = list(X.tl)
            src = d_xT if layer == 0 else d_xs
            for tc in range(4):
                t0, n = TCH[tc]
                load("sp", X, X[:, :, t0:t0 + n], src[:, :, t0:t0 + n], tiles=[X.tl[tc]])
            AR = Region(adas_base[0], 2 * KC * 512 * 2)
            AR.after = S.collect([t_ for b_ in ADAS.bufs for t_ in b_.tl])
            wos = [AR.sb("WO%d" % oc, [128, 8, 128], BF16) for oc in range(8)]
            for oc in range(8):
                load("pool", wos[oc], wos[oc][:], d_wo[oc])
            lagq = []
            for tc in range(5):
                t0, n = TCH[tc]
                for oc in range(8):
                    pb = BK.next()
                    for j in range(8):
                        mm(pb[:, :n], wos[oc][:, j, :], QT[:, j, t0:t0 + n], j == 0, j == 7,
                           wos[oc].tl + [QT.tl[(j * 2) * 5 + tc], QT.tl[(j * 2 + 1) * 5 + tc]], pb.tl[0])
                    if tc == 4:
                        stt(CX[:, oc, :], pb[:, :n], gate_ap(layer, 0, 1, oc), CX[:, oc, :], ALU.mult, ALU.add,
                            pb.tl + [MOD.tl[layer]] + CX.tl, CX.tl)
                    else:
                        stt(X[:, oc, t0:t0 + n], pb[:, :n], gate_ap(layer, 0, 0, oc), X[:, oc, t0:t0 + n],
                            ALU.mult, ALU.add, pb.tl + [MOD.tl[layer], X.tl[tc]], [X.tl[tc]])
                if post_chunk is not None:
                    lagq.append(tc)
                    if len(lagq) > 1:
                        post_chunk(lagq.pop(0))
            while lagq:
                post_chunk(lagq.pop(0))

        def ffn_layer(layer, with_ctx, do_norm=True, post_chunk=None):
            PH.reset()
            chunks = list(range(5)) if with_ctx else list(range(4))
            ntok = T if with_ctx else L
            if do_norm:
                norm_mod(layer, 1, chunks)
            GS = 6
            A = PH.sb("A", [128, GS, T], BF16, ntiles=GS * 5)
            G = PH.sb("G", [128, T + 4], F32, ntiles=5)
            ACR = Ring([PH.sb("AC%d" % i, [128, T], F32, ntiles=5) for i in range(2)])
            WUR = Ring([PH.sb("WU%d" % i, [128, KC, 256], BF16) for i in range(3)])
            WD = PH.sb("WD", [128, GS, D], BF16, ntiles=GS)
            S.op("dve", [], G.tl, lambda e: e.memset(G[:, 0:1], 0.0))
            S.op("dve", [], G.tl, lambda e: e.memset(G[:, L + 1:L + 3], 0.0))
            S.op("dve", [], G.tl, lambda e: e.memset(G[:, T + 3:T + 4], 0.0))
            groups = []
            f = 0
            while f < NFC:
                groups.append(list(range(f, min(f + GS, NFC))))
                f += GS
            wu = {}

            def wuload(fc):
                s_ = WUR.next()
                load("pool", s_, s_[:], d_wup[layer, fc])
                wu[fc] = s_
            wuload(0)
            wuload(1)
            for grp in groups:
                for gi, fc in enumerate(grp):
                    if fc + 2 < NFC:
                        wuload(fc + 2)
                    load("pool", WD, WD[:, gi, :], d_wdn[layer, fc], tiles=[WD.tl[gi]])
                    ac = ACR.next()
                    ubanks = []
                    for tc in chunks:
                        t0, n = TCH[tc]
                        g0 = t0 + 1 if tc < 4 else t0 + 3
                        pg = BK.next()
                        for kc in range(KC):
                            mm(pg[:, :n], wu[fc][:, kc, 0:128], H[:, kc, t0:t0 + n], kc == 0, kc == KC - 1,
                               wu[fc].tl + [H.tl[tc]], pg.tl[0])
                        act(ac[:, t0:t0 + n], pg[:, :n], AF.Identity, pg.tl + FCW.tl, [ac.tl[tc]],
                            bias=FCW[:, layer, fc, 3:4], scale=FCW[:, layer, fc, 1:2])
                        S.op("act", pg.tl, [G.tl[tc]], lambda e, pg=pg, g0=g0, n=n: e.copy(out=G[:, g0:g0 + n], in_=pg[:, :n]))
                        pu = BK.next()
                        for kc in range(KC):
                            mm(pu[:, :n], wu[fc][:, kc, 128:256], H[:, kc, t0:t0 + n], kc == 0, kc == KC - 1,
                               wu[fc].tl + [H.tl[tc]], pu.tl[0])
                        ubanks.append(pu)
                        if len(ubanks) >= 3 or tc == chunks[-1]:
                            pass
                        if tc >= 1:
                            conv_chunk(layer, fc, gi, tc - 1, G, ac, A, ubanks[tc - 1], with_ctx)
                    conv_chunk(layer, fc, gi, chunks[-1], G, ac, A, ubanks[-1], with_ctx)
                last_grp = grp is groups[-1]
                lagq = []
                for tc in chunks:
                    t0, n = TCH[tc]
                    for oc in range(8):
                        pb = BK.next()
                        for gi, fc in enumerate(grp):
                            mm(pb[:, :n], WD[:, gi, oc * 128:(oc + 1) * 128], A[:, gi, t0:t0 + n],
                               gi == 0, gi == len(grp) - 1, [WD.tl[gi], A.tl[gi * 5 + tc]], pb.tl[0])
                        if tc == 4:
                            stt(CX[:, oc, :], pb[:, :n], gate_ap(layer, 1, 1, oc), CX[:, oc, :], ALU.mult, ALU.add,
                                pb.tl + [MOD.tl[layer]] + CX.tl, CX.tl)
                        else:
                            stt(X[:, oc, t0:t0 + n], pb[:, :n], gate_ap(layer, 1, 0, oc), X[:, oc, t0:t0 + n],
                                ALU.mult, ALU.add, pb.tl + [MOD.tl[layer], X.tl[tc]], [X.tl[tc]])
                    if last_grp and post_chunk is not None:
                        lagq.append(tc)
                        if len(lagq) > 1:
                            post_chunk(lagq.pop(0))
                while lagq:
                    post_chunk(lagq.pop(0))

        def conv_chunk(layer, fc, gi, tc, G, ac, A, pu, with_ctx):
            t0, n = TCH[tc]
            g0 = t0 + 1 if tc < 4 else t0 + 3
            gt = [G.tl[k] for k in (tc - 1, tc, tc + 1) if 0 <= k <= 4]
            stt(ac[:, t0:t0 + n], G[:, g0 - 1:g0 - 1 + n], FCW[:, layer, fc, 0:1], ac[:, t0:t0 + n],
                ALU.mult, ALU.add, gt + FCW.tl + [ac.tl[tc]], [ac.tl[tc]])
            stt(ac[:, t0:t0 + n], G[:, g0 + 1:g0 + 1 + n], FCW[:, layer, fc, 2:3], ac[:, t0:t0 + n],
                ALU.mult, ALU.add, gt + FCW.tl + [ac.tl[tc]], [ac.tl[tc]])
            act(ac[:, t0:t0 + n], ac[:, t0:t0 + n], AF.Silu, [ac.tl[tc]], [ac.tl[tc]])
            tt("dve", A[:, gi, t0:t0 + n], ac[:, t0:t0 + n], pu[:, :n], ALU.mult, [ac.tl[tc]] + pu.tl, [A.tl[gi * 5 + tc]])

        def ssd_layer(layer, do_norm=True, post_chunk=None):
            PH.reset()
            for tc in range(4):
                t0, n = TCH[tc]
                S.dma("sp", [X.tl[tc]], [], lambda e, t0=t0, n=n: e.dma_start(out=d_xs[:, :, t0:t0 + n], in_=X[:, :, t0:t0 + n]))
            if do_norm:
                norm_mod(layer, 0, range(5))
            XR.reset()
            CXR = Region(cx_base, KC * LC * 4)
            CXR.after = S.collect(CX.tl)
            BKs = Ring(banks[0:3]); BKC = Ring([banks[1], banks[3]]); BKY = Ring(banks[4:5]); BKQ = Ring([banks[2], banks[5], banks[6], banks[7]]); BKst = Ring(banks[0:1]); BKsb = Ring([banks[3], banks[4]])
            TRI = [CSTF[:, 0, :], CSTF[:, 1, :]]
            MNEG = [CSTF[:, 3, :], CSTF[:, 4, :]]
            ONES1 = CSTF[:, 2, :]
            IDB = CST[:, 3, :]
            XST = XR.sb("XST", [128, 2, T], BF16, ntiles=2)
            BT = XR.sb("BT", [128, T], BF16)
            CT = XR.sb("CT", [128, L], BF16)
            SZ = XR.sb("SZ", [128, 2, L], BF16, ntiles=2)
            TM = XR.sb("TM", [128, 18, 384], BF16, ntiles=18)
            VG = XR.sb("VG", [128, 2, L], BF16)
            WDT = XR.sb("WDT", [128, KC, 64], BF16)
            DTB = XR.sb("DTB", [128, 64], F32)
            ALG = XR.sb("ALG", [128, 64], F32)
            SCW = XR.sb("SCW", [128, NG, 4, 6], F32)
            SDD = XR.sb("SDD", [128, NG, 2], F32)
            HS = [XR.sb("HS%d" % d_, [128, 256], F32, ntiles=4) for d_ in range(2)]
            DTH = CXR.sb("DTH", [128, 18, 64], BF16)
            DTL = CXR.sb("DTL", [128, 18, 64], BF16)
            TRIB = [CST[:, 5, :], CST[:, 6, :]]
            MNEGB = [CST[:, 7, :], CST[:, 8, :]]
            DTT = PH.sb("DTT", [128, 18, 64], F32)
            DTA = PH.sb("DTA", [128, 18, 64], F32)
            ETA = PH.sb("ETA", [128, 18, 64], F32)
            ACS = PH.sb("ACS", [128, 18, 64], F32)
            DDA = PH.sb("DDA", [128, 18, 64], F32)
            G2 = PH.sb("G2", [128, T + 8], BF16, ntiles=5)
            DGR = Ring([PH.sb("DG%d" % i, [128, 5, 128], BF16) for i in range(2)])
            WIR = Ring([PH.sb("WI%d" % i, [128, KC, 128], BF16) for i in range(3)])
            HIN = [PH.sb("HIN%d" % d_, [128, 16, 256], BF16, ntiles=16) for d_ in range(2)]

            def ring(name, shape, dtype, n=2):
                return Ring([PH.sb("%s%d" % (name, i), shape, dtype) for i in range(n)])
            XWR = ring("XW", [128, 256], BF16, 4)
            XDTR = [ring("XDT%d" % d_, [128, 256], BF16, 3) for d_ in range(2)]
            LMR = ring("LM", [128, 4, 128], BF16, 2)
            WR = ring("W", [128, 4, 128], BF16, 6)
            EBR = ring("EB", [128, 4, 128], BF16, 4)
            CPR = ring("CP", [128, 4, 128], BF16, 6)
            YVR = ring("YV", [128, 2, 128], F32, 2)

            load("pool", WDT, WDT[:], d_wdt)
            load("sp", DTB, DTB[:], d_dtb)
            load("sp", ALG, ALG[:], d_alg)
            load("sp", SCW, SCW[:], d_scw)
            load("sp", SDD, SDD[:], d_sdd)
            for blk in range(18):
                pb = BKs.next()
                tc = min(blk // 4, 4)
                for kc in range(KC):
                    mm(pb[:, :64], H[:, kc, blk * 128:(blk + 1) * 128], WDT[:, kc, :], kc == 0, kc == KC - 1,
                       [H.tl[tc]] + WDT.tl, pb.tl[0])
                tt("dve", DTT[:, blk, :], pb[:, :64], DTB[:], ALU.add, pb.tl + DTB.tl, DTT.tl)
            act(ETA[:], DTT[:], AF.Exp, DTT.tl, ETA.tl)
            act(DTT[:], ETA[:], AF.Ln, ETA.tl + EPSB.tl, DTT.tl, bias=EPSB[:, 1:2])
            act(ALG[:], ALG[:], AF.Exp, ALG.tl, ALG.tl)
            stt(DTA[:], DTT[:], -1.0, ALG[:].unsqueeze(1).to_broadcast([128, 18, 64]), ALU.mult, ALU.mult,
                DTT.tl + ALG.tl, DTA.tl)
            S.op("dve", DTA.tl, DTH.tl, lambda e: e.tensor_copy(out=DTH[:], in_=DTA[:]))
            tt("dve", ETA[:], DTA[:], DTH[:], ALU.subtract, DTA.tl + DTH.tl, ETA.tl)
            S.op("dve", ETA.tl, DTL.tl, lambda e: e.tensor_copy(out=DTL[:], in_=ETA[:]))
            for blk in range(18):
                pa = BKs.next()
                for d_ in range(2):
                    cs = slice(d_ * 32, d_ * 32 + 32)
                    mm(pa[:, d_ * 32:d_ * 32 + 32], TRI[d_], DTA[:, blk, cs], True, True, DTA.tl + CSTF.tl, pa.tl[0])
                    mm(pa[:, 64 + d_ * 32:96 + d_ * 32], ONES1, DTA[:, blk, cs], True, True, DTA.tl + CSTF.tl, pa.tl[0])
                S.op("act", pa.tl, ACS.tl, lambda e, pa=pa, blk=blk: e.mul(out=ACS[:, blk, :], in_=pa[:, 0:64], mul=-1.0))
                S.op("act", pa.tl, ETA.tl, lambda e, pa=pa, blk=blk: e.copy(out=ETA[:, blk, :], in_=pa[:, 64:128]))
            tt("dve", DDA[:], ETA[:], ACS[:], ALU.add, ETA.tl + ACS.tl, DDA.tl)
            act(DDA[:], DDA[:], AF.Exp, DDA.tl, DDA.tl)
            tt("dve", DDA[:], DDA[:], DTT[:], ALU.mult, DDA.tl + DTT.tl, DDA.tl)
            act(ETA[:], ETA[:], AF.Exp, ETA.tl, ETA.tl)
            dump("DTT", DTT[:], DTT.tl, [128, 18, 64], F32)
            S.op("dve", [], G2.tl, lambda e: e.memset(G2[:, 0:2], 0.0))
            S.op("dve", [], G2.tl, lambda e: e.memset(G2[:, L + 2:L + 6], 0.0))
            S.op("dve", [], G2.tl, lambda e: e.memset(G2[:, T + 6:T + 8], 0.0))

            def conv5(g, w, tc, dg, dst_ap, dst_tiles):
                t0, n = TCH[tc]
                g0 = t0 + 2 if tc < 4 else t0 + 6
                gt = [G2.tl[k] for k in (tc - 1, tc, tc + 1) if 0 <= k <= 4]
                pcv = BKs.next()
                for k in range(5):
                    mm(pcv[:, :n], dg[:, k, :], G2[:, g0 + k - 2:g0 + k - 2 + n], k == 0, k == 4,
                       dg.tl + gt, pcv.tl[0])
                act(dst_ap, pcv[:, :n], AF.Silu, pcv.tl + SCW.tl, dst_tiles, bias=SCW[:, g, w, 5:6], nowaw=True)

            def xview(blk):
                return TM[:, blk, 0:256].rearrange("p (r d) -> p r d", d=64)

            def state_step(g, blk, d_, xw, bkr=None):
                cols = slice(d_ * 32 + g * 4, d_ * 32 + g * 4 + 4)
                ps_ = (bkr or BKst).next()
                mm(ps_[:, :256], TM[:, blk, 256:384], xw[:], True, True, [TM.tl[blk]] + xw.tl, ps_.tl[0])
                hs = HS[d_]
                for r in range(4):
                    stt(hs[:, r * 64:(r + 1) * 64], hs[:, r * 64:(r + 1) * 64], ETA[:, blk, cols.start + r:cols.start + r + 1],
                        ps_[:, r * 64:(r + 1) * 64], ALU.mult, ALU.add, [hs.tl[r]] + ETA.tl + ps_.tl, [hs.tl[r]])

            def make_xw(g, blk, d_):
                cols = slice(d_ * 32 + g * 4, d_ * 32 + g * 4 + 4)
                xw = XWR.next()
                tt("pool", xw[:].rearrange("p (r d) -> p r d", d=64), xview(blk),
                   DDA[:, blk, cols].unsqueeze(2).to_broadcast([128, 4, 64]), ALU.mult,
                   [TM.tl[blk]] + DDA.tl, xw.tl)
                return xw

            def stage_a(g, c):
                tsl = slice(c * 128, (c + 1) * 128)
                R = {0: {}, 1: {}}
                pc = BKC.next()
                mm(pc[:, :128], BT[:, tsl], CT[:, tsl], True, True, BT.tl + CT.tl, pc.tl[0])
                pqs = {}
                for d_ in range(2):
                    c0 = d_ * 32 + g * 4
                    pq = BKQ.next()
                    for r in range(4):
                        o_ = pq[:, r * 128:(r + 1) * 128]
                        mm(o_, DTH[:, c, c0 + r:c0 + r + 1].to_broadcast([128, 128]), TRIB[d_], r == 0, False,
                           DTH.tl + CST.tl, pq.tl[0], sgc=True)
                        mm(o_, DTL[:, c, c0 + r:c0 + r + 1].to_broadcast([128, 128]), TRIB[d_], False, True,
                           DTL.tl + CST.tl, pq.tl[0], sgc=True)
                    pqs[d_] = pq
                for d_ in range(2):
                    eb = EBR.next()
                    act(eb[:], pqs[d_][:, 0:512].rearrange("p (r q) -> p r q", q=128), AF.Exp, pqs[d_].tl, eb.tl)
                    R[d_]["eb"] = eb
                R["pqs"] = pqs
                R["pc"] = pc
                return R

            def stage_a2(g, c, R):
                tsl = slice(c * 128, (c + 1) * 128)
                pqs = R["pqs"]
                pc = R["pc"]
                for d_ in range(2):
                    for r in range(4):
                        mm(pqs[d_][:, r * 128:(r + 1) * 128], IDB, MNEGB[d_], False, True, CST.tl, pqs[d_].tl[0], sgc=True)
                for d_ in range(2):
                    cols = slice(d_ * 32 + g * 4, d_ * 32 + g * 4 + 4)
                    xdt = XDTR[d_].next()
                    tt("pool", xdt[:].rearrange("p (r d) -> p r d", d=64), xview(c),
                       DTT[:, c, cols].unsqueeze(2).to_broadcast([128, 4, 64]), ALU.mult,
                       [TM.tl[c]] + DTT.tl, xdt.tl)
                    R[d_]["xdt"] = xdt
                for d_ in range(2):
                    c0 = d_ * 32 + g * 4
                    lm = LMR.next()
                    for r in range(4):
                        act(lm[:, r, :], pqs[d_][:, r * 128:(r + 1) * 128], AF.Exp, pqs[d_].tl + ACS.tl, lm.tl,
                            bias=ACS[:, c, c0 + r:c0 + r + 1], nowaw=True)
                    R[d_]["lm"] = lm
                for d_ in range(2):
                    cp = CPR.next()
                    tt("pool", cp[:], R[d_]["eb"][:], CT[:, tsl].unsqueeze(1).to_broadcast([128, 4, 128]), ALU.mult,
                       CT.tl + R[d_]["eb"].tl, cp.tl)
                    R[d_]["cp"] = cp
                for d_ in range(2):
                    w_ = WR.next()
                    tt("dve", w_[:], R[d_]["lm"][:], pc[:, :128].unsqueeze(1).to_broadcast([128, 4, 128]), ALU.mult,
                       R[d_]["lm"].tl + pc.tl, w_.tl)
                    R[d_]["w"] = w_

            def stage_b(g, c, R):
                tsl = slice(c * 128, (c + 1) * 128)
                py = BKY.next()
                for r in range(4):
                    yo = py[0:64, r * 128:(r + 1) * 128]
                    for d_ in range(2):
                        P = R[d_]
                        mm(yo, P["xdt"][:, r * 64:(r + 1) * 64], P["w"][:, r, :], d_ == 0, False,
                           P["xdt"].tl + P["w"].tl, py.tl[0])
                        mm(yo, HIN[d_][:, c, r * 64:(r + 1) * 64], P["cp"][:, r, :], False, d_ == 1,
                           [HIN[d_].tl[c]] + P["cp"].tl, py.tl[0])
                yv = YVR.next()
                for r in range(4):
                    c2, hp = r // 2, (r % 2) * 64
                    stt(yv[hp:hp + 64, c2, :], XST[hp:hp + 64, c2, tsl], SDD[hp:hp + 64, g, c2:c2 + 1],
                        py[0:64, r * 128:(r + 1) * 128], ALU.mult, ALU.add,
                        [XST.tl[c2]] + SDD.tl + py.tl, yv.tl, nowaw=True)
                tt("dve", VG[:, :, tsl], yv[:], SZ[:, :, tsl], ALU.mult, yv.tl + SZ.tl, VG.tl, nowaw=True)

            WOS = []
            for g in range(NG):
                wsl = {}
                ordr = [2, 3, 4, 0, 1, 5]

                def wiload(k):
                    s_ = WIR.next()
                    load("pool", s_, s_[:], d_win[g, ordr[k]])
                    wsl[k] = s_

                def inproj_gen(ks):
                    for k in ks:
                        i = ordr[k]
                        if k + 2 < 6:
                            wiload(k + 2)
                        slab = wsl[k]
                        if i < 2:
                            for tc in range(4):
                                t0, n = TCH[tc]
                                pb = BKs.next()
                                for kc in range(KC):
                                    mm(pb[:, :n], slab[:, kc, :], H[:, kc, t0:t0 + n], kc == 0, kc == KC - 1,
                                       slab.tl + [H.tl[tc]], pb.tl[0])
                                act(SZ[:, i, t0:t0 + n], pb[:, :n], AF.Silu, pb.tl, [SZ.tl[i]], nowaw=True)
                                yield
                            continue
                        w = i - 2
                        chunks = list(range(5)) if w < 3 else list(range(4))
                        if w < 2:
                            dstf = lambda t0, n, w=w: (XST[:, w, t0:t0 + n], [XST.tl[w]])
                        elif w == 2:
                            dstf = lambda t0, n: (BT[:, t0:t0 + n], BT.tl)
                        else:
                            dstf = lambda t0, n: (CT[:, t0:t0 + n], CT.tl)
                        ac = DGR.next()
                        for kk in range(5):
                            ts("dve", ac[:, kk, :], IDB, SCW[:, g, w, kk:kk + 1], None, ALU.mult, None,
                               CST.tl + SCW.tl, ac.tl)
                        for tc in chunks:
                            t0, n = TCH[tc]
                            g0 = t0 + 2 if tc < 4 else t0 + 6
                            pg = BKs.next()
                            for kc in range(KC):
                                mm(pg[:, :n], slab[:, kc, :], H[:, kc, t0:t0 + n], kc == 0, kc == KC - 1,
                                   slab.tl + [H.tl[tc]], pg.tl[0])
                            S.op("act", pg.tl, [G2.tl[tc]],
                                 lambda e, pg=pg, g0=g0, n=n: e.copy(out=G2[:, g0:g0 + n], in_=pg[:, :n]))
                            if tc >= 1:
                                tp0, np_ = TCH[tc - 1]
                                da, dtl = dstf(tp0, np_)
                                conv5(g, w, tc - 1, ac, da, dtl)
                            yield
                        tp0, np_ = TCH[chunks[-1]]
                        da, dtl = dstf(tp0, np_)
                        conv5(g, w, chunks[-1], ac, da, dtl)
                        yield

                def state_gen():
                    S.op("dve", [], HS[1].tl, lambda e: e.memset(HS[1][:], 0.0))
                    S.op("dve", [], HS[0].tl, lambda e: e.memset(HS[0][:], 0.0))
                    seq = [(blk, 1) for blk in [17, 16] + list(range(15, -1, -1))] + [(16, 0), (17, 0)]
                    xws = {}
                    for i in range(2):
                        xws[i] = make_xw(g, seq[i][0], seq[i][1])
                    for i, (blk, d_) in enumerate(seq):
                        if i + 2 < len(seq):
                            xws[i + 2] = make_xw(g, seq[i + 2][0], seq[i + 2][1])
                        if d_ == 1 and blk < 16:
                            S.op("act", HS[1].tl, [HIN[1].tl[blk]],
                                 lambda e, blk=blk: e.copy(out=HIN[1][:, blk, :], in_=HS[1][:]))
                        state_step(g, blk, d_, xws.pop(i), BKsb)
                        yield

                wiload(0)
                wiload(1)
                for _ in inproj_gen([0, 1, 2]):
                    pass
                if g == NG - 1:
                    pass
                for blk in range(18):
                    pb = BKs.next()
                    tsl = slice(blk * 128, (blk + 1) * 128)
                    mm(pb[:, 0:128], XST[:, 0, tsl], IDB, True, True, [XST.tl[0]] + CST.tl, pb.tl[0])
                    mm(pb[:, 128:256], XST[:, 1, tsl], IDB, True, True, [XST.tl[1]] + CST.tl, pb.tl[0])
                    mm(pb[:, 256:384], BT[:, tsl], IDB, True, True, BT.tl + CST.tl, pb.tl[0])
                    S.op("act", pb.tl, [TM.tl[blk]], lambda e, pb=pb, blk=blk: e.copy(out=TM[:, blk, :], in_=pb[:, 0:384]))
                g2 = inproj_gen([3, 4, 5])
                gs = state_gen()
                done2 = dones = False
                while not (done2 and dones):
                    if not done2:
                        try:
                            next(g2)
                        except StopIteration:
                            done2 = True
                    for _ in range(2):
                        if not dones:
                            try:
                                next(gs)
                            except StopIteration:
                                dones = True
                if g == NG - 1:
                    HR = Region(h_base, KC * T * 2)
                    HR.after = S.collect(H.tl)
                    WOS.extend([HR.sb("WOS%d" % oc, [128, 16, 128], BF16) for oc in range(8)])
                    for oc in range(8):
                        load("pool", WOS[oc], WOS[oc][:], d_wout[oc])
                xwf = {0: make_xw(g, 0, 0), 1: make_xw(g, 1, 0)}
                Rs = {}
                for c in range(18):
                    if c < 16:
                        if c + 2 < 16:
                            xwf[c + 2] = make_xw(g, c + 2, 0)
                        S.op("act", HS[0].tl, [HIN[0].tl[c]], lambda e, c=c: e.copy(out=HIN[0][:, c, :], in_=HS[0][:]))
                        state_step(g, c, 0, xwf.pop(c))
                        Rs[c] = stage_a(g, c)
                    if 1 <= c <= 16:
                        stage_a2(g, c - 1, Rs[c - 1])
                    if c >= 2:
                        stage_b(g, c - 2, Rs.pop(c - 2))
                if g == 0:
                    dump("XST", XST[:], XST.tl, [128, 2, T], BF16)
                    dump("VG", VG[:], VG.tl, [128, 2, L], BF16)
                    dump("HIN0", HIN[0][:], HIN[0].tl, [128, 16, 256], BF16)
                    dump("HIN1", HIN[1][:], HIN[1].tl, [128, 16, 256], BF16)
                S.dma("sp", VG.tl, [], lambda e, g=g: e.dma_start(out=d_vs[:, 2 * g:2 * g + 2, :], in_=VG[:]))

            vs_ev = S.collect(VG.tl)
            PH.reset()
            ev = XR.reset()
            S.fence(X.tl, ev)
            XR.tiles = list(X.tl)
            for tc in range(4):
                t0, n = TCH[tc]
                load("sp", X, X[:, :, t0:t0 + n], d_xs[:, :, t0:t0 + n], tiles=[X.tl[tc]])
            SNW = PH.sb("SNW", [128, 16], F32)
            load("sp", SNW, SNW[:], d_snw)
            VRR = Ring([PH.sb("VR%d" % i, [128, 16, 512], BF16) for i in range(2)])
            for oc in range(8):
                tt("dve", WOS[oc][:], WOS[oc][:], SNW[:].unsqueeze(2).to_broadcast([128, 16, 128]), ALU.mult,
                   WOS[oc].tl + SNW.tl, WOS[oc].tl)
            lagq = []
            for tc in range(4):
                t0, n = TCH[tc]
                vr = VRR.next()
                S.fence(vr.tl, vs_ev)
                load("sp", vr, vr[:], d_vs[:, :, t0:t0 + n])
                pb = BK.next()
                for kc in range(16):
                    sq = SQR.next()
                    act(sq[:, :n], vr[:, kc, :], AF.Square, vr.tl, sq.tl)
                    mm(pb[:, :n], CST[:, 4, :], sq[:, :n], kc == 0, kc == 15, sq.tl + CST.tl, pb.tl[0])
                rs = rstd_from_ms(pb, n)
                for oc in range(8):
                    po = BK.next()
                    for kc in range(16):
                        mm(po[:, :n], WOS[oc][:, kc, :], vr[:, kc, :], kc == 0, kc == 15, WOS[oc].tl + vr.tl, po.tl[0])
                    tm = TMR.next()
                    tt("dve", tm[:, :n], po[:, :n], rs[:, :n], ALU.mult, po.tl + rs.tl, tm.tl)
                    stt(X[:, oc, t0:t0 + n], tm[:, :n], gate_ap(layer, 0, 0, oc), X[:, oc, t0:t0 + n],
                        ALU.mult, ALU.add, tm.tl + [MOD.tl[layer], X.tl[tc]], [X.tl[tc]])
                if tc == 0:
                    pass
                if post_chunk is not None:
                    lagq.append(tc)
            S.fence(H.tl, S.collect([t_ for w_ in WOS for t_ in w_.tl]))
            while lagq:
                post_chunk(lagq.pop(0))
            S.fence(H.tl, S.collect([t_ for w_ in WOS for t_ in w_.tl]))

        def out_chunk(tc):
            t0, n = TCH[tc]
            S.dma("sp", [X.tl[tc]], [], lambda e: e.dma_start(out=d_out[:, :, t0:t0 + n], in_=X[:, :, t0:t0 + n]))

        attention_layer(0, post_chunk=lambda tc: norm_mod(0, 1, [tc]))
        dump("X1", X[:], X.tl, [128, KC, L], F32)
        dump("C1", CX[:], CX.tl, [128, KC, LC], F32)
        if n_layers > 1:
            ffn_layer(0, True, do_norm=False, post_chunk=lambda tc: norm_mod(1, 0, [tc]))
            dump("C2", CX[:], CX.tl, [128, KC, LC], F32)
            ssd_layer(1, do_norm=False, post_chunk=lambda tc: norm_mod(1, 1, [tc]))
            dump("X3", X[:], X.tl, [128, KC, L], F32)
            ffn_layer(1, False, do_norm=False, post_chunk=out_chunk)
        else:
            ffn_layer(0, True, do_norm=False)
            for tc in range(4):
                out_chunk(tc)
        for ch in S.dma_ch:
            if ch["count"] > 0:
                S._wait("sp", ch["sem"], ch["count"])
        with nc.Block() as block:
            S.emit(block)
    return nc


_PROG = {}


def _host_layout(inputs):
    f = np.float32
    g = lambda k: np.asarray(inputs[k], dtype=f)
    x, c, ctx, c_ctx = g("x"), g("c"), g("ctx"), g("c_ctx")
    B = x.shape[0]
    sh = {}
    ada_w = g("ada_w")
    sh["ada_w"] = np.ascontiguousarray(ada_w.reshape(2, KC, 128, 12, 512).transpose(0, 3, 2, 1, 4))
    sh["ada_b"] = np.ascontiguousarray(g("ada_b").reshape(2, 48, 128).transpose(2, 0, 1))
    nw = np.stack([g("norm_mix_w"), g("norm_ffn_w")], 0)
    sh["nw"] = np.ascontiguousarray(nw.reshape(2, 2, KC, 128).transpose(3, 0, 1, 2))
    wqkv = g("attn_w_qkv")[0]
    cols = []
    for (ha, hb) in PAIRS:
        cols.append(np.concatenate([np.arange(ha * 64, ha * 64 + 64), np.arange(hb * 64, hb * 64 + 64)]))
    cols = np.stack(cols)
    wq = wqkv[:, cols]
    sh["wq"] = np.ascontiguousarray(wq.reshape(KC, 128, 8, 128).transpose(2, 1, 0, 3))
    wk = wqkv[:, 1024:1280].reshape(KC, 128, 2, 128)
    sh["wk"] = np.ascontiguousarray(wk.transpose(2, 1, 0, 3))
    sh["wv"] = np.ascontiguousarray(wqkv[:, 1280:1536].reshape(KC, 128, 256).transpose(1, 0, 2))
    qn, kn = g("attn_q_norm")[0], g("attn_k_norm")[0]
    sh["qkn"] = np.ascontiguousarray(np.stack([np.tile(qn, 2), np.tile(kn, 2)], 1))
    wo = g("attn_w_o")[0]
    wo_r = wo[cols.reshape(-1)].reshape(8, 128, 8, 128)
    sh["wo"] = np.ascontiguousarray(wo_r.transpose(2, 1, 0, 3))
    t = np.arange(L)
    row = (t // 64).astype(f)
    col = (t % 64).astype(f)
    inv = (np.float32(10000.0) ** (-np.arange(0, 32, 2, dtype=f) / np.float32(32))).astype(f)
    ang = np.concatenate([row[:, None] * inv, col[:, None] * inv], -1).astype(f)
    cos, sin = np.cos(ang).astype(f), np.sin(ang).astype(f)
    pidx = (np.arange(128) % 64) // 2
    sh["cs"] = np.ascontiguousarray(np.stack([cos[:, pidx].T, sin[:, pidx].T], 1))
    cst = np.zeros((128, 9, 128), f)
    cst[:, 0, :] = 1.0 / 1024
    for hh in range(2):
        cst[hh * 64:(hh + 1) * 64, 1, hh * 64:(hh + 1) * 64] = 1.0 / 64
    for i in range(64):
        cst[2 * i + 1, 2, 2 * i] = -1.0
        cst[2 * i, 2, 2 * i + 1] = 1.0
    cst[:, 3, :] = np.eye(128, dtype=f)
    cst[:, 4, :] = 1.0 / 2048
    cst[:, 5, :] = np.triu(np.ones((128, 128), f))
    cst[:, 6, :] = np.tril(np.ones((128, 128), f))
    cst[:, 7, :] = (cst[:, 5, :] - 1.0) * 1.0e5
    cst[:, 8, :] = (cst[:, 6, :] - 1.0) * 1.0e5
    cstf = np.zeros((128, 5, 128), f)
    cstf[:, 0, :] = np.triu(np.ones((128, 128), f))
    cstf[:, 1, :] = np.tril(np.ones((128, 128), f))
    cstf[:, 2, :] = 1.0
    cstf[:, 3, :] = (cstf[:, 0, :] - 1.0) * 1.0e5
    cstf[:, 4, :] = (cstf[:, 1, :] - 1.0) * 1.0e5
    sh["cstf"] = cstf
    sh["cst"] = cst
    wup = g("ffn_w_up")
    gpart = wup[:, :, :DFF].reshape(2, KC, 128, NFC, 128)
    upart = wup[:, :, DFF:].reshape(2, KC, 128, NFC, 128)
    wu = np.concatenate([gpart, upart], -1)
    sh["wup"] = np.ascontiguousarray(wu.transpose(0, 3, 2, 1, 4))
    sh["wdn"] = np.ascontiguousarray(g("ffn_w_down").reshape(2, NFC, 128, D))
    fcw = np.concatenate([g("ffn_conv_w"), g("ffn_conv_b")[:, None, :]], 1)
    sh["fcw"] = np.ascontiguousarray(fcw.reshape(2, 4, NFC, 128).transpose(3, 0, 2, 1))
    w_in = g("ssd_w_in")[0]
    win = np.zeros((NG, 6, 128, KC, 128), f)
    for gg in range(NG):
        starts = [gg * 256, gg * 256 + 128, 2048 + gg * 256, 2048 + gg * 256 + 128, 4096 + gg * 128, 5120 + gg * 128]
        for i, c0 in enumerate(starts):
            win[gg, i] = w_in[:, c0:c0 + 128].reshape(KC, 128, 128).transpose(1, 0, 2)
    sh["win"] = win
    sh["wdt"] = np.ascontiguousarray(w_in[:, 6144:6208].reshape(KC, 128, 64).transpose(1, 0, 2))
    cw = np.concatenate([g("ssd_conv_w")[0], g("ssd_conv_b")[0][None, :]], 0)
    scw = np.zeros((128, NG, 4, 6), f)
    for gg in range(NG):
        for w_, c0 in enumerate([gg * 256, gg * 256 + 128, 2048 + gg * 128, 3072 + gg * 128]):
            scw[:, gg, w_, :] = cw[:, c0:c0 + 128].T
    sh["scw"] = scw
    dtb = np.concatenate([g("ssd_dt_bias_f")[0].reshape(32), g("ssd_dt_bias_b")[0].reshape(32)])
    alg = np.concatenate([g("ssd_a_log_f")[0].reshape(32), g("ssd_a_log_b")[0].reshape(32)])
    sh["dtb"] = np.ascontiguousarray(np.broadcast_to(dtb[None, :], (128, 64)))
    sh["alg"] = np.ascontiguousarray(np.broadcast_to(alg[None, :], (128, 64)))
    dsk = g("ssd_d")[0]
    sdd = np.zeros((128, NG, 2), f)
    for gg in range(NG):
        for c2 in range(2):
            sdd[0:64, gg, c2] = dsk[gg, c2 * 2]
            sdd[64:128, gg, c2] = dsk[gg, c2 * 2 + 1]
    sh["sdd"] = sdd
    sh["snw16"] = np.ascontiguousarray(g("ssd_norm_w")[0].reshape(16, 128).T)
    sh["wout"] = np.ascontiguousarray(g("ssd_w_out")[0].reshape(16, 128, 8, 128).transpose(2, 1, 0, 3))
    maps = []
    for b in range(B):
        m = dict(sh)
        m["xT"] = np.ascontiguousarray(x[b].T.reshape(KC, 128, L).transpose(1, 0, 2))
        m["ctxT"] = np.ascontiguousarray(ctx[b].T.reshape(KC, 128, LC).transpose(1, 0, 2))
        m["cc"] = np.ascontiguousarray(np.stack([c[b].reshape(KC, 128).T, c_ctx.reshape(KC, 128).T], -1))
        maps.append(m)
    return maps


def kernel(**inputs):
    if "nc" not in _PROG:
        _PROG["nc"] = build_program()
    nc = _PROG["nc"]
    maps = _host_layout(inputs)
    res = run_bass_kernel_spmd(nc, maps, core_ids=list(range(len(maps))))
    _PROG["res"] = res
    outs = []
    for r in res.results:
        o = np.asarray(r["outT"])
        outs.append(o.transpose(2, 1, 0).reshape(L, D))
    return np.stack(outs, 0).astype(np.float32)
```
